# Optimizing a Trainium2 kernel written in Bass

```python
import math
import jax, jax.numpy as jnp
from jax import lax
import numpy as np

D_MODEL = 1024
BATCH = 2
SEQ = 8192
DEPTH = 4
DEC_BATCH = 128
DEC_SEQ = 1
PAST_LEN = 8192
PAGE_SIZE = 128

N_MIXERS = 2
N_A = (DEPTH + 1) // 2
N_B = DEPTH // 2
D_RNN = D_MODEL
LRU_BLOCKS = 4
LRU_BW = D_RNN // LRU_BLOCKS
LRU_C = 8.0
CONV_W = 4
N_HEADS = 16
N_KV = 4
GROUP = N_HEADS // N_KV
HEAD_DIM = 64
Q_DIM = N_HEADS * HEAD_DIM
KV_DIM = N_KV * HEAD_DIM
QKV_DIM = Q_DIM + 2 * KV_DIM
WINDOW = 128
BLOCK = WINDOW
D_FF = 4 * D_MODEL
RMS_EPS = 1e-6
NEG_INF = -1e30

kernel_name = "hybrid_rglru_swa_sink_decoder_step"


def rms_norm(x, g):
    xf = x.astype(jnp.float32)
    y = xf * lax.rsqrt(jnp.mean(xf * xf, axis=-1, keepdims=True) + RMS_EPS)
    return (y * g.astype(jnp.float32)).astype(x.dtype)


def causal_conv(x, prev, w, b):
    t = x.shape[1]
    xp = jnp.concatenate([prev.astype(x.dtype), x], axis=1)
    y = xp[:, 0:t] * w[0]
    for k in range(1, CONV_W):
        y = y + xp[:, k:k + t] * w[k]
    return y + b, xp[:, t:]


def rg_lru(x, h0, w_a, b_a, w_x, b_x, lam):
    n, t, _ = x.shape
    f32 = jnp.float32
    xf = x.astype(f32)
    xb = xf.reshape(n, t, LRU_BLOCKS, LRU_BW)
    r = jax.nn.sigmoid(jnp.einsum('ntkb,kbc->ntkc', xb, w_a.astype(f32)).reshape(n, t, D_RNN) + b_a.astype(f32))
    i = jax.nn.sigmoid(jnp.einsum('ntkb,kbc->ntkc', xb, w_x.astype(f32)).reshape(n, t, D_RNN) + b_x.astype(f32))
    log_a = -LRU_C * r * jax.nn.softplus(-lam.astype(f32))
    a = jnp.exp(log_a)
    u = jnp.sqrt(-jnp.expm1(2.0 * log_a)) * (i * xf)

    def step(h, au):
        a_t, u_t = au
        h = a_t * h + u_t
        return h, h

    h_last, hs = lax.scan(step, h0.astype(f32), (a.swapaxes(0, 1), u.swapaxes(0, 1)))
    return hs.swapaxes(0, 1).astype(x.dtype), h_last.astype(h0.dtype)


def recurrent_block(x, h0, conv_prev, w_in, conv_w, conv_b, w_a, b_a, w_x, b_x, lam, w_out):
    u = x @ w_in
    gate, xr = jnp.split(u, 2, axis=-1)
    xc, conv_new = causal_conv(xr, conv_prev, conv_w, conv_b)
    y, h_new = rg_lru(xc, h0, w_a, b_a, w_x, b_x, lam)
    return (y * jax.nn.gelu(gate)) @ w_out, h_new, conv_new


def qkv_heads(x, w_qkv, q_g, k_g):
    n, t, _ = x.shape
    q, k, v = jnp.split(x @ w_qkv, [Q_DIM, Q_DIM + KV_DIM], axis=-1)
    q = rms_norm(q.reshape(n, t, N_KV, GROUP, HEAD_DIM), q_g)
    k = rms_norm(k.reshape(n, t, N_KV, HEAD_DIM), k_g)
    v = v.reshape(n, t, N_KV, HEAD_DIM)
    return q, k, v


def alibi_slopes():
    h = jnp.arange(1, N_HEADS + 1, dtype=jnp.float32)
    return jnp.exp2(-8.0 * h / N_HEADS).reshape(N_KV, GROUP)


def sink_attention(q, k, v, dist, mask, sinks):
    s = jnp.einsum('...qkgd,...skd->...kgqs', q, k).astype(jnp.float32) * (HEAD_DIM ** -0.5)
    s = s - alibi_slopes()[:, :, None, None] * dist.astype(jnp.float32)
    s = jnp.where(mask, s, NEG_INF)
    sink = sinks.astype(jnp.float32).reshape(N_KV, GROUP)[:, :, None, None]
    m = jnp.maximum(jnp.max(s, axis=-1, keepdims=True), sink)
    e = jnp.exp(s - m)
    p = e / (jnp.sum(e, axis=-1, keepdims=True) + jnp.exp(sink - m))
    return jnp.einsum('...kgqs,...skd->...qkgd', p.astype(v.dtype), v)


def swa_prompt(x, w_qkv, q_g, k_g, sinks, w_out, w_buf):
    n, t, _ = x.shape
    nb = t // BLOCK
    q, k, v = qkv_heads(x, w_qkv, q_g, k_g)
    qb = q.reshape(n, nb, BLOCK, N_KV, GROUP, HEAD_DIM)

    def with_prev(z):
        zb = z.reshape(n, nb, BLOCK, N_KV, HEAD_DIM)
        prev = jnp.pad(zb, ((0, 0), (1, 0), (0, 0), (0, 0), (0, 0)))[:, :-1]
        return jnp.concatenate([prev, zb], axis=2)

    qi = jnp.arange(BLOCK)[:, None]
    si = jnp.arange(2 * BLOCK)[None, :]
    dist = qi + BLOCK - si
    key_pos = jnp.arange(nb)[:, None, None] * BLOCK + si[None] - BLOCK
    mask = (dist >= 0) & (dist < WINDOW) & (key_pos >= 0)
    o = sink_attention(qb, with_prev(k), with_prev(v), dist, mask[:, None, None], sinks)
    y = o.reshape(n, t, Q_DIM) @ w_out
    return y, k[:, t - w_buf:], v[:, t - w_buf:]


def swa_sample(x, k_cache, v_cache, w_qkv, q_g, k_g, sinks, w_out):
    n, s, _ = x.shape
    w_buf = k_cache.shape[1]
    q, k, v = qkv_heads(x, w_qkv, q_g, k_g)
    k_all = jnp.concatenate([k_cache.astype(k.dtype), k], axis=1)
    v_all = jnp.concatenate([v_cache.astype(v.dtype), v], axis=1)
    q_pos = PAST_LEN + jnp.arange(s)
    k_pos = jnp.concatenate([PAST_LEN - w_buf + jnp.arange(w_buf), q_pos])
    dist = q_pos[:, None] - k_pos[None, :]
    mask = (dist >= 0) & (dist < WINDOW)
    o = sink_attention(q, k_all, v_all, dist, mask, sinks)
    y = o.reshape(n, s, Q_DIM) @ w_out
    return y, k_all[:, s:], v_all[:, s:]


def sq_relu_mlp(x, w_up, w_down):
    return jnp.square(jax.nn.relu(x @ w_up)) @ w_down


def setup_inputs(seed: int = 0) -> dict:
    key = jax.random.key(seed)
    ks = jax.random.split(key, 24)
    f32 = jnp.float32
    w_buf = min(WINDOW, PAST_LEN)

    def nrm(k, shape, scale):
        return jax.random.normal(k, shape, f32) * scale

    u = jax.random.uniform(ks[15], (N_A, D_RNN), f32, minval=0.9, maxval=0.999)
    a_base = u ** (1.0 / LRU_C)
    lam = jnp.log(a_base) - jnp.log1p(-a_base)
    return {
        "x_prompt": nrm(ks[0], (BATCH, SEQ, D_MODEL), 1.0),
        "x_sample": nrm(ks[1], (DEC_BATCH, DEC_SEQ, D_MODEL), 1.0),
        "state_rglru_h": nrm(ks[2], (N_A, DEC_BATCH, D_RNN), 0.5),
        "state_rglru_conv": nrm(ks[3], (N_A, DEC_BATCH, CONV_W - 1, D_RNN), 1.0),
        "cache_swa_k": nrm(ks[4], (N_B, DEC_BATCH, w_buf, N_KV, HEAD_DIM), 1.0),
        "cache_swa_v": nrm(ks[5], (N_B, DEC_BATCH, w_buf, N_KV, HEAD_DIM), 1.0),
        "norm_mix_g": 1.0 + nrm(ks[6], (DEPTH, D_MODEL), 0.02),
        "norm_mlp_g": 1.0 + nrm(ks[7], (DEPTH, D_MODEL), 0.02),
        "lru_w_in": nrm(ks[8], (N_A, D_MODEL, 2 * D_RNN), D_MODEL ** -0.5),
        "lru_conv_w": nrm(ks[9], (N_A, CONV_W, D_RNN), CONV_W ** -0.5),
        "lru_conv_b": nrm(ks[10], (N_A, D_RNN), 0.01),
        "lru_w_a": nrm(ks[11], (N_A, LRU_BLOCKS, LRU_BW, LRU_BW), LRU_BW ** -0.5),
        "lru_b_a": nrm(ks[12], (N_A, D_RNN), 0.01),
        "lru_w_x": nrm(ks[13], (N_A, LRU_BLOCKS, LRU_BW, LRU_BW), LRU_BW ** -0.5),
        "lru_b_x": nrm(ks[14], (N_A, D_RNN), 0.01),
        "lru_lambda": lam,
        "lru_w_out": nrm(ks[16], (N_A, D_RNN, D_MODEL), D_RNN ** -0.5),
        "attn_w_qkv": nrm(ks[17], (N_B, D_MODEL, QKV_DIM), D_MODEL ** -0.5),
        "attn_q_norm": 1.0 + nrm(ks[18], (N_B, HEAD_DIM), 0.02),
        "attn_k_norm": 1.0 + nrm(ks[19], (N_B, HEAD_DIM), 0.02),
        "attn_sinks": nrm(ks[20], (N_B, N_HEADS), 0.5),
        "attn_w_out": nrm(ks[21], (N_B, Q_DIM, D_MODEL), Q_DIM ** -0.5),
        "mlp_w_up": nrm(ks[22], (DEPTH, D_MODEL, D_FF), D_MODEL ** -0.5),
        "mlp_w_down": nrm(ks[23], (DEPTH, D_FF, D_MODEL), 0.7 * D_FF ** -0.5),
    }


def reference(x_prompt, x_sample, state_rglru_h, state_rglru_conv, cache_swa_k, cache_swa_v,
              norm_mix_g, norm_mlp_g,
              lru_w_in, lru_conv_w, lru_conv_b, lru_w_a, lru_b_a, lru_w_x, lru_b_x, lru_lambda, lru_w_out,
              attn_w_qkv, attn_q_norm, attn_k_norm, attn_sinks, attn_w_out,
              mlp_w_up, mlp_w_down):
    n_p = x_prompt.shape[0]
    w_buf = cache_swa_k.shape[2]
    yp, ys = x_prompt, x_sample
    h_p_list, c_p_list, k_p_list, v_p_list = [], [], [], []
    h_s_list, c_s_list, k_s_list, v_s_list = [], [], [], []
    for layer in range(DEPTH):
        j = layer // N_MIXERS
        hp = rms_norm(yp, norm_mix_g[layer])
        hs = rms_norm(ys, norm_mix_g[layer])
        if layer % N_MIXERS == 0:
            params = (lru_w_in[j], lru_conv_w[j], lru_conv_b[j], lru_w_a[j], lru_b_a[j],
                      lru_w_x[j], lru_b_x[j], lru_lambda[j], lru_w_out[j])
            h0 = jnp.zeros((n_p, D_RNN), state_rglru_h.dtype)
            c0 = jnp.zeros((n_p, CONV_W - 1, D_RNN), hp.dtype)
            mp, h_p, c_p = recurrent_block(hp, h0, c0, *params)
            ms, h_s, c_s = recurrent_block(hs, state_rglru_h[j], state_rglru_conv[j], *params)
            h_p_list.append(h_p); c_p_list.append(c_p)
            h_s_list.append(h_s); c_s_list.append(c_s)
        else:
            mp, k_p, v_p = swa_prompt(hp, attn_w_qkv[j], attn_q_norm[j], attn_k_norm[j],
                                      attn_sinks[j], attn_w_out[j], w_buf)
            ms, k_s, v_s = swa_sample(hs, cache_swa_k[j], cache_swa_v[j], attn_w_qkv[j], attn_q_norm[j],
                                      attn_k_norm[j], attn_sinks[j], attn_w_out[j])
            k_p_list.append(k_p); v_p_list.append(v_p)
            k_s_list.append(k_s); v_s_list.append(v_s)
        yp = yp + mp
        ys = ys + ms
        yp = yp + sq_relu_mlp(rms_norm(yp, norm_mlp_g[layer]), mlp_w_up[layer], mlp_w_down[layer])
        ys = ys + sq_relu_mlp(rms_norm(ys, norm_mlp_g[layer]), mlp_w_up[layer], mlp_w_down[layer])
    return (yp, ys,
            jnp.stack(h_p_list), jnp.stack(c_p_list), jnp.stack(k_p_list), jnp.stack(v_p_list),
            jnp.stack(h_s_list), jnp.stack(c_s_list), jnp.stack(k_s_list), jnp.stack(v_s_list))
```

```python
import numpy as np
import concourse.bass as bass
import concourse.mybir as mybir
from concourse.bass_utils import run_bass_kernel_spmd

F32, BF16 = mybir.dt.float32, mybir.dt.bfloat16
AF = mybir.ActivationFunctionType
ALU = mybir.AluOpType

D = 1024
SEQ = 8192
CH = 1024
ST = 512
NCHUNK = SEQ // CH
NS = 16
EPS = 1e-6
GA = 1.5957691216057308
GB = GA * 0.044715
NSLOT = 3


class Prog:
    def __init__(self, nc):
        self.nc = nc
        self.ops = []
        self.last_w = {}
        self.readers = {}
        self.small_ctx = True

    def op(self, eng, fn, r=(), w=(), dma=False, small=None):
        i = len(self.ops)
        deps = set()
        for k in r:
            if k in self.last_w:
                deps.add(self.last_w[k])
        for k in w:
            if k in self.last_w:
                deps.add(self.last_w[k])
            deps.update(self.readers.get(k, ()))
        for k in r:
            self.readers.setdefault(k, []).append(i)
        for k in w:
            self.last_w[k] = i
            self.readers[k] = []
        deps.discard(i)
        self.ops.append(dict(eng=eng, fn=fn, deps=deps, dma=dma, small=self.small_ctx if small is None else small))
        return i

    def emit(self):
        nc = self.nc
        ops = self.ops
        engs = {"pe": nc.tensor, "act": nc.scalar, "dve": nc.vector, "pool": nc.gpsimd, "sp": nc.sync}
        need = [False] * len(ops)
        for i, o in enumerate(ops):
            best = {}
            keep = []
            for d in o["deps"]:
                od = ops[d]
                if od["dma"]:
                    keep.append(d)
                elif od["eng"] == o["eng"] and not o["dma"] and not od["small"] and not o["small"]:
                    continue
                else:
                    e = od["eng"]
                    if e not in best or best[e] < d:
                        best[e] = d
            keep.extend(best.values())
            o["deps"] = keep
            for d in keep:
                need[d] = True
        sem_e = {e: nc.semaphore("s_" + e).__enter__() for e in engs}
        pools = {"sp": list(range(0, 28)), "pool": list(range(28, 40))}
        ND = 40
        dsem = [nc.semaphore("d%d" % i).__enter__() for i in range(ND)]
        dcount = [0] * ND
        dnext = {"sp": 0, "pool": 0}
        sig = {}
        cnt = {e: 0 for e in engs}
        waited = {}

        def wait(e, sem, key, val):
            if waited.get((e, key), 0) < val:
                engs[e].wait_ge(sem, val)
                waited[(e, key)] = val

        for i, o in enumerate(ops):
            e = o["eng"]
            for d in o["deps"]:
                sem, key, val = sig[d]
                wait(e, sem, key, val)
            if o["dma"]:
                s = pools[e][dnext[e] % len(pools[e])]
                dnext[e] += 1
                if dcount[s] > 0:
                    wait(e, dsem[s], "d%d" % s, 16 * dcount[s])
                dcount[s] += 1
                inst = o["fn"]()
                inst.then_inc(dsem[s], 16)
                sig[i] = (dsem[s], "d%d" % s, 16 * dcount[s])
            else:
                inst = o["fn"]()
                if need[i]:
                    cnt[e] += 1
                    inst.then_inc(sem_e[e], 1)
                    sig[i] = (sem_e[e], e, cnt[e])
        self.stats = dict(cnt=dict(cnt), nops=len(ops), dma=sum(dcount))
        for s in range(ND):
            if dcount[s] > 0:
                wait("sp", dsem[s], "d%d" % s, 16 * dcount[s])


CFG = dict(nchunk=NCHUNK, layers=(0, 1, 2, 3), sample=True, mlp=True, mixer=True)


def build_program():
    nc = bass.Bass("TRN2", target_bir_lowering=False)
    P = Prog(nc)

    def din(name, shape, dt=F32):
        return nc.dram_tensor(name, list(shape), dt, kind="ExternalInput").ap()

    def dout(name, shape, dt=F32):
        return nc.dram_tensor(name, list(shape), dt, kind="ExternalOutput").ap()

    xT = din("xT", [D, SEQ])
    xsT = din("xsT", [D, NS])
    h0T = din("h0T", [2, 128, 8, NS])
    cvT = din("cvT", [2, 3, 128, 8, NS])
    ck = din("ck", [2, NS, 128, 256])
    cv = din("cv", [2, NS, 128, 256])
    w_in = din("w_in", [2, D, 2048])
    w_gt = din("w_gt", [2, 128, 16, 256])
    w_lo = din("w_lo", [2, D, D])
    w_q = din("w_q", [2, D, D])
    w_kv = din("w_kv", [2, D, 768])
    w_ao = din("w_ao", [2, D, D])
    w_up = din("w_up", [4, D, 4096])
    w_dn = din("w_dn", [4, 4096, D])
    NPRM = 4 * 8 * 2 + 2 * 8 * 4 + 4 * 2 * 8 + 2 + 2 + 2 * 64 + 16
    prm_d = din("prm", [128, NPRM])
    etab_d = din("etab", [128, 2 * 4 * 512])
    sink_d = din("sinks", [1, 2 * 16])

    yT = dout("yT", [D, SEQ])
    ysT = dout("ysT", [D, NS])
    hp_o = dout("hp", [2, 128, 8])
    cp_o = dout("cp", [2, 128, 8, 3])
    kp_o = dout("kp", [2, 64, 4, 128])
    vp_o = dout("vp", [2, 128, 256])
    hs_o = dout("hs", [2, 128, 8, NS])
    cs_o = dout("cs", [2, 3, 128, 8, NS])
    ks_o = dout("ks", [2, NS, 128, 256])
    vs_o = dout("vs", [2, NS, 128, 256])

    def sb(name, shape, dt=F32):
        return nc.sbuf_tensor("sb_" + name, list(shape), dt).__enter__()

    x_sb = sb("x_sb", [128, 8, CH])
    xn_sb = sb("xn_sb", [128, 8, CH], BF16)
    xs_sb = sb("xs_sb", [128, 8, NS])
    xns_sb = sb("xns_sb", [128, 8, NS], BF16)
    wslot = [sb("wslot%d" % i, [128, 8, 1024], BF16) for i in range(NSLOT)]
    S8a = sb("S8a", [128, 8, ST], BF16)
    S8b = sb("S8b", [128, 8, ST], BF16)
    F4 = [sb("F4_%d" % i, [128, 2, ST]) for i in range(4)]
    xr_sb = [sb("xr_sb%d" % i, [128, 2, ST + 3]) for i in range(1)]
    wgates = sb("wgates", [128, 16, 256], BF16)
    xcb_sb = sb("xcb", [128, 2, ST], BF16)
    p_sb = [sb("p_sb%d" % i, [128, 2, ST], BF16) for i in range(2)]
    KT_sb = sb("KT", [128, 4, 128 + CH], BF16)
    V_sb = sb("V", [128, 1 + CH // 128, 256], BF16)
    kthalo = [sb("kthalo%d" % j, [128, 4, 128], BF16) for j in range(2)]
    vhalo = [sb("vhalo%d" % j, [128, 256], BF16) for j in range(2)]
    etab = sb("etab", [128, 2, 4, 512], BF16)
    prm = sb("prm", [128, NPRM])
    ones_bf = sb("ones_bf", [128, 128], BF16)
    blk_bf = sb("blk_bf", [128, 128], BF16)
    ident_bf = sb("ident_bf", [128, 128], BF16)
    ident_d = din("ident", [128, 128])
    sinkb = sb("sinkb", [64, 2, 16])
    cpar = sb("cpar", [128, 2, 2, 8])
    gq8 = sb("gq8", [128, 2])
    hst = sb("hst", [128, 2, 8])
    xtail = sb("xtail", [128, 2, 8, 3])
    tmp1 = sb("tmp1", [128, ST])
    rstd = sb("rstd", [128, ST])
    sq_bf = S8b
    kfin = sb("kfin", [128, 4, 128])
    vfin = sb("vfin", [128, 256])
    h0_sb = sb("h0_sb", [128, 2, 8, NS])
    cv_sb = sb("cv_sb", [128, 2, 3, 8, NS])
    Knew = [sb("Knew%d" % i, [128, 256]) for i in range(2)]
    Vnew = [sb("Vnew%d" % i, [128, 256]) for i in range(2)]
    Kdup = sb("Kdup", [128, 4, 2, 64], BF16)
    KTs = sb("KTs", [128, 4, 128], BF16)
    Vb = sb("Vb", [128, 256], BF16)
    ktok = sb("ktok", [NS, 256])
    vtok = sb("vtok", [NS, 256])
    ksq = sb("ksq", [NS, 256])
    kss = sb("kss", [NS, 4])
    es_sb = sb("es_sb", [128, 16])
    ps_s = sb("ps_s", [128, 16], BF16)
    recs = sb("recs", [64, 16])
    S8s = sb("S8s", [128, 8, NS], BF16)
    QTs = sb("QTs", [128, 8, NS], BF16)
    OTs = sb("OTs", [128, 8, NS], BF16)
    hs_small = sb("hs_small", [128, 8, NS], BF16)

    psb = [nc.psum_tensor("ps%d" % i, [128, 512], F32).__enter__() for i in range(5)]
    psA = nc.psum_tensor("psA", [128, 512], F32).__enter__()
    psB = nc.psum_tensor("psB", [128, 512], F32).__enter__()
    psT = nc.psum_tensor("psT", [128, 4, 128], BF16).__enter__()
    ps_i = [0]

    def newps():
        i = ps_i[0]
        ps_i[0] = (i + 1) % len(psb)
        return psb[i], ("ps", i)

    off = {}
    o = 0
    for name, n in [("gmix", 32), ("gmlp", 32), ("convw", 64), ("convb", 16), ("ba", 16), ("bx", 16),
                    ("lam", 16), ("gq", 2), ("gk", 2), ("gkrow", 128), ("es", 16)]:
        off[name] = o
        o += n
    assert o == NPRM

    def pcol(name, idx):
        c = off[name] + idx
        return prm[:, c:c + 1]

    def dma(eng, out, in_, r, w):
        e = nc.gpsimd if eng == "pool" else nc.sync
        P.op(eng, lambda: e.dma_start(out=out, in_=in_), r=r, w=w, dma=True)

    def act(out, in_, func, r, w, bias=None, scale=1.0):
        kw = {}
        if bias is not None:
            kw["bias"] = bias
        P.op("act", lambda: nc.scalar.activation(out=out, in_=in_, func=func, scale=scale, **kw), r=r, w=w)

    def tt(out, in0, in1, op, r, w, eng="dve"):
        e = nc.vector if eng == "dve" else nc.gpsimd
        P.op(eng, lambda: e.tensor_tensor(out=out, in0=in0, in1=in1, op=op), r=r, w=w)

    def ts(out, in0, s1, s2, op0, op1, r, w):
        P.op("dve", lambda: nc.vector.tensor_scalar(out=out, in0=in0, scalar1=s1, scalar2=s2, op0=op0, op1=op1),
             r=r, w=w)

    def stt(out, in0, scalar, in1, op0, op1, r, w):
        P.op("dve", lambda: nc.vector.scalar_tensor_tensor(out=out, in0=in0, scalar=scalar, in1=in1,
                                                            op0=op0, op1=op1), r=r, w=w)

    def mm(out, lhsT, rhs, start, stop, r, w):
        P.op("pe", lambda: nc.tensor.matmul(out, lhsT=lhsT, rhs=rhs, start=start, stop=stop), r=r, w=w)

    dumps = []

    def dump(name, ap, key):
        if not CFG.get("dbg"):
            return
        shp = list(ap.shape)
        d = nc.dram_tensor("dbg_" + name, shp, ap.dtype, kind="ExternalOutput").ap()
        dma("sp", d, ap, r=[key], w=[])
        dumps.append("dbg_" + name)

    wsched = []
    wissued = [0]
    wreleased = set()

    def wpump():
        while wissued[0] < len(wsched):
            i = wissued[0]
            if i >= NSLOT and (i - NSLOT) not in wreleased:
                break
            src = wsched[i]
            slot = wslot[i % NSLOT]
            ncols = src.shape[1]
            dma("pool", slot[:, :, 0:ncols], src.rearrange("(k p) n -> p k n", p=128), r=[], w=[("w", i % NSLOT)])
            wissued[0] += 1

    def wrel(i):
        wreleased.add(i)
        wpump()

    wuse = [0]

    def wnext():
        i = wuse[0]
        wuse[0] += 1
        wpump()
        assert wissued[0] > i, "weight slot deadlock"
        return wslot[i % NSLOT], ("w", i % NSLOT), i

    def layer_blocks(layer):
        j = layer // 2
        bl = []
        if layer % 2 == 0:
            bl += [w_in[j, :, 0:1024], w_in[j, :, 1024:2048], w_lo[j]]
        else:
            bl += [w_q[j], w_kv[j], w_ao[j]]
        for f in range(4):
            bl += [w_up[layer, :, f * 1024:(f + 1) * 1024], w_dn[layer, f * 1024:(f + 1) * 1024, :]]
        return bl

    for ci in range(CFG["nchunk"]):
        for layer in CFG["layers"]:
            bl = layer_blocks(layer)
            if not CFG["mixer"]:
                bl = bl[3:]
            if not CFG["mlp"]:
                bl = bl[:3]
            wsched.extend(bl)

    dma("sp", prm[:, :], prm_d, r=[], w=["prm"])
    dma("pool", etab[:, :, :, :].rearrange("p a g n -> p (a g n)"), etab_d, r=[], w=["etab"])
    dma("sp", sinkb[:, :, :].rearrange("p a n -> p (a n)"), sink_d.partition_broadcast(64), r=[], w=["sinkb"])
    dma("sp", xs_sb[:, :, :], xsT.rearrange("(k p) n -> p k n", p=128), r=[], w=["xs"])
    dma("sp", h0_sb[:, :, :, :], h0T.rearrange("j p k n -> p j k n"), r=[], w=["h0"])
    for j in range(2):
        dma("sp", cv_sb[:, j, :, :, :], cvT[j].rearrange("t p k n -> p t k n"), r=[], w=[("cvs", j)])
    P.op("dve", lambda: nc.vector.memset(ones_bf[:, :], 1.0), w=["ones"])
    P.op("dve", lambda: nc.vector.memset(blk_bf[:, :], 0.0), w=["blk"])
    P.op("dve", lambda: nc.vector.memset(blk_bf[0:64, 0:64], 1.0), w=["blk"])
    P.op("dve", lambda: nc.vector.memset(blk_bf[64:128, 64:128], 1.0), w=["blk"])
    P.op("dve", lambda: nc.vector.memset(hst[:, :, :], 0.0), w=["hst"])
    P.op("dve", lambda: nc.vector.memset(xtail[:, :, :, :], 0.0), w=["xtail"])
    dma("pool", ident_bf[:, :], ident_d, r=[], w=["ident"])
    spy = sb("spy", [128, 16])
    spl = sb("spl", [128, 16])
    sps = sb("sps", [128, 16])
    spm = sb("spm", [128, 16])
    lam_all = prm[:, off["lam"]:off["lam"] + 16]
    act(spy[:, :], lam_all, AF.Exp, r=["prm"], w=["spy"], scale=-1.0)
    act(spl[:, :], spy[:, :], AF.Ln, r=["spy"], w=["spl"], bias=1.0)
    ts(sps[:, :], spy[:, :], -0.2, 0.25, ALU.mult, ALU.add, r=["spy"], w=["sps"])
    for cst in (1.0 / 3.0, 0.5, 1.0):
        tt(sps[:, :], sps[:, :], spy[:, :], ALU.mult, r=["sps", "spy"], w=["sps"])
        ts(sps[:, :], sps[:, :], -1.0, cst, ALU.mult, ALU.add, r=["sps"], w=["sps"])
    tt(sps[:, :], sps[:, :], spy[:, :], ALU.mult, r=["sps", "spy"], w=["sps"])
    ts(spm[:, :], spy[:, :], 0.05, None, ALU.is_lt, ALU.bypass, r=["spy"], w=["spm"])
    tt(sps[:, :], sps[:, :], spl[:, :], ALU.subtract, r=["sps", "spl"], w=["sps"])
    tt(sps[:, :], sps[:, :], spm[:, :], ALU.mult, r=["sps", "spm"], w=["sps"])
    tt(spl[:, :], spl[:, :], sps[:, :], ALU.add, r=["sps", "spl"], w=["spl"])
    for j in range(2):
        ts(cpar[:, j, 0, :], spl[:, j * 8:(j + 1) * 8], -8.0, None, ALU.mult, ALU.bypass, r=["spl"], w=["cpar"])
        ts(cpar[:, j, 1, :], spl[:, j * 8:(j + 1) * 8], -16.0, None, ALU.mult, ALU.bypass, r=["spl"], w=["cpar"])
    ts(gq8[:, :], prm[:, off["gq"]:off["gq"] + 2], 0.125, None, ALU.mult, ALU.bypass, r=["prm"], w=["gq8"])
    act(sinkb[:, :, :], sinkb[:, :, :], AF.Exp, r=["sinkb"], w=["sinkb"])

    def rmsnorm(xsrc, xkey, xdst, dkey, N, gname, layer):
        act(sq_bf[:, :, 0:N], xsrc, AF.Square, r=[xkey], w=["S8b"])
        ps, pk = newps()
        for k in range(8):
            mm(ps[:, 0:N], ones_bf[:, :], sq_bf[:, k, 0:N], k == 0, k == 7, r=["ones", "S8b"], w=[pk])
        act(tmp1[:, 0:N], ps[:, 0:N], AF.Ln, r=[pk], w=["tmp1"], bias=EPS, scale=1.0 / D)
        act(rstd[:, 0:N], tmp1[:, 0:N], AF.Exp, r=["tmp1"], w=["rstd"], scale=-0.5)
        for k in range(8):
            stt(xdst[:, k, :], xsrc[:, k, :], pcol(gname, layer * 8 + k), rstd[:, 0:N], ALU.mult, ALU.mult,
                r=[xkey, "prm", "rstd"], w=[dkey])

    def linear(wt, wkey, col0, src, skey, N, kchunks=8):
        ps, pk = newps()
        for k in range(kchunks):
            mm(ps[:, 0:N], wt[:, k, col0:col0 + 128], src[:, k, :], k == 0, k == kchunks - 1,
               r=[wkey, skey], w=[pk])
        return ps, pk

    def add_resid(xt, xkey, oc, ps, pk, N):
        tt(xt[:, oc, :], ps[:, 0:N], xt[:, oc, :], ALU.add, r=[pk, xkey], w=[xkey])

    def tiles_for_chunk(ci):
        tl = []
        for st in range(CH // ST):
            tl.append(dict(x=x_sb[:, :, st * ST:(st + 1) * ST], xk="x", xn=xn_sb[:, :, st * ST:(st + 1) * ST],
                           xnk=("xn", st), N=ST, samp=False, st=st))
        if ci == 0 and CFG["sample"]:
            tl.append(dict(x=xs_sb[:, :, :], xk="xs", xn=xns_sb[:, :, :], xnk="xns", N=NS, samp=True, st=0))
        return tl

    def mlp(layer, tiles):
        for t in tiles:
            P.small_ctx = t["samp"]
            rmsnorm(t["x"], t["xk"], t["xn"], t["xnk"], t["N"], "gmlp", layer)
        for f in range(4):
            wu, wuk, wui = wnext()
            wd, wdk, wdi = wnext()
            for t in tiles:
                P.small_ctx = t["samp"]
                N = t["N"]
                hbuf, hk = (S8b, "S8b") if not t["samp"] else (hs_small, "hs_small")
                for oc in range(8):
                    ps, pk = linear(wu, wuk, oc * 128, t["xn"], t["xnk"], N)
                    tmp = F4[oc % 2][:, 0, 0:N]
                    tk = ("F4", oc % 2)
                    act(tmp, ps[:, 0:N], AF.Relu, r=[pk], w=[tk])
                    tt(hbuf[:, oc, 0:N], tmp, tmp, ALU.mult, r=[tk], w=[hk], eng="pool")
                for oc in range(8):
                    ps, pk = linear(wd, wdk, oc * 128, hbuf[:, :, 0:N], hk, N)
                    add_resid(t["x"], t["xk"], oc, ps, pk, N)
            wrel(wui)
            wrel(wdi)

    def rglru(layer, tiles, ci):
        j = layer // 2
        for t in tiles:
            P.small_ctx = t["samp"]
            rmsnorm(t["x"], t["xk"], t["xn"], t["xnk"], t["N"], "gmix", layer)
        wg, wgk, wgi = wnext()
        wx, wxk, wxi = wnext()
        wo, wok, woi = wnext()
        wgtk = "wgates"
        dma("pool", wgates[:, :, :], w_gt[j], r=[], w=[wgtk])
        wgt_v = wgates[:, :, :].rearrange("p (t b k) n -> p t b k n", t=2, b=4, k=2)
        for t in tiles:
            P.small_ctx = t["samp"]
            N = t["N"]
            samp = t["samp"]
            dbg0 = (ci == 0 and t["st"] == 0 and not samp and layer == 0)
            G, Gk = (S8a, "S8a") if not samp else (S8s, "S8s")
            Y, Yk = (S8b, "S8b") if not samp else (hs_small, "hs_small")
            for oc in range(8):
                ps, pk = linear(wg, wgk, oc * 128, t["xn"], t["xnk"], N)
                f0 = F4[oc % 2][:, 0, 0:N]
                f1 = F4[oc % 2][:, 1, 0:N]
                fk = ("F4", oc % 2)
                act(f0, ps[:, 0:N], AF.Square, r=[pk], w=[fk])
                ts(f0, f0, 0.5 * GB, 0.5 * GA, ALU.mult, ALU.add, r=[fk], w=[fk])
                tt(f0, f0, ps[:, 0:N], ALU.mult, r=[fk, pk], w=[fk])
                act(f1, f0, AF.Tanh, r=[fk], w=[fk])
                stt(G[:, oc, 0:N], f1, 1.0, ps[:, 0:N], ALU.add, ALU.mult, r=[fk, pk], w=[Gk])
            if dbg0:
                dump("G", G[:, :, 0:N], Gk)
                dump("xn", t["xn"], t["xnk"])
                dump("cpar", cpar[:, :, :, :], "cpar")
            for b in range(4):
                xr = xr_sb[0]
                xrk = ("xr", 0)
                xc, r_, i_, a_, s_, h_ = F4[0], F4[1], F4[2], F4[3], F4[1], F4[0]
                fk = [("F4", 0), ("F4", 1), ("F4", 2), ("F4", 3), ("F4", 1), ("F4", 0)]
                for l in range(2):
                    oc = 2 * b + l
                    ps, pk = linear(wx, wxk, oc * 128, t["xn"], t["xnk"], N)
                    cw = lambda tap: pcol("convw", (j * 8 + oc) * 4 + tap)
                    cb = pcol("convb", j * 8 + oc)
                    if not samp:
                        P.op("dve", lambda xr=xr, l=l, oc=oc: nc.vector.tensor_copy(
                            out=xr[:, l, 0:3], in_=xtail[:, j, oc, :]), r=["xtail"], w=[xrk], small=True)
                        act(xr[:, l, 3:3 + N], ps[:, 0:N], AF.Copy, r=[pk], w=[xrk])
                        P.op("dve", lambda xr=xr, l=l, oc=oc, N=N: nc.vector.tensor_copy(
                            out=xtail[:, j, oc, :], in_=xr[:, l, N:N + 3]), r=[xrk], w=["xtail"], small=True)
                        ts(xc[:, l, 0:N], xr[:, l, 0:N], cw(0), cb, ALU.mult, ALU.add, r=[xrk, "prm"], w=[fk[0]])
                        for tap in range(1, 4):
                            stt(xc[:, l, 0:N], xr[:, l, tap:tap + N], cw(tap), xc[:, l, 0:N], ALU.mult, ALU.add,
                                r=[xrk, "prm", fk[0]], w=[fk[0]])
                    else:
                        act(xr[:, l, 0:N], ps[:, 0:N], AF.Copy, r=[pk], w=[xrk])
                        ts(xc[:, l, 0:N], xr[:, l, 0:N], cw(3), cb, ALU.mult, ALU.add, r=[xrk, "prm"], w=[fk[0]])
                        for tap in range(3):
                            stt(xc[:, l, 0:N], cv_sb[:, j, tap, oc, :], cw(tap), xc[:, l, 0:N], ALU.mult, ALU.add,
                                r=[("cvs", j), "prm", fk[0]], w=[fk[0]])
                        dma("sp", cs_o[j, 2, :, oc, :], xr[:, l, 0:N], r=[xrk], w=[])
                    act(xcb_sb[:, l, 0:N], xc[:, l, 0:N], AF.Copy, r=[fk[0]], w=["xcb"])
                for l in range(2):
                    oc = 2 * b + l
                    psr, pkr = newps()
                    psi, pki = newps()
                    for k in range(2):
                        mm(psr[:, 0:N], wgt_v[:, 0, b, k, l * 128:(l + 1) * 128], xcb_sb[:, k, 0:N], k == 0, k == 1,
                           r=[wgtk, "xcb"], w=[pkr])
                    for k in range(2):
                        mm(psi[:, 0:N], wgt_v[:, 1, b, k, l * 128:(l + 1) * 128], xcb_sb[:, k, 0:N], k == 0, k == 1,
                           r=[wgtk, "xcb"], w=[pki])
                    act(r_[:, l, 0:N], psr[:, 0:N], AF.Sigmoid, r=[pkr, "prm"], w=[fk[1]], bias=pcol("ba", j * 8 + oc))
                    act(i_[:, l, 0:N], psi[:, 0:N], AF.Sigmoid, r=[pki, "prm"], w=[fk[2]], bias=pcol("bx", j * 8 + oc))
                for l in range(2):
                    oc = 2 * b + l
                    act(a_[:, l, 0:N], r_[:, l, 0:N], AF.Exp, r=[fk[1], "cpar"], w=[fk[3]], scale=cpar[:, j, 0, oc:oc + 1])
                    act(s_[:, l, 0:N], r_[:, l, 0:N], AF.Exp, r=[fk[1], "cpar"], w=[fk[4]], scale=cpar[:, j, 1, oc:oc + 1])
                for l in range(2):
                    ts(s_[:, l, 0:N], s_[:, l, 0:N], 1.0, None, ALU.min, ALU.bypass, r=[fk[4]], w=[fk[4]])
                    act(s_[:, l, 0:N], s_[:, l, 0:N], AF.Sqrt, r=[fk[4]], w=[fk[4]], bias=1.0, scale=-1.0)
                    if dbg0 and b == 0:
                        dump("a%d" % l, a_[:, l, 0:N], fk[3])
                        dump("s%d" % l, s_[:, l, 0:N], fk[4])
                        dump("i%d" % l, i_[:, l, 0:N], fk[2])
                        dump("xc%d" % l, xc[:, l, 0:N], fk[0])
                for l in range(2):
                    oc = 2 * b + l
                    tt(i_[:, l, 0:N], i_[:, l, 0:N], xc[:, l, 0:N], ALU.mult, r=[fk[2], fk[0]], w=[fk[2]])
                    tt(i_[:, l, 0:N], i_[:, l, 0:N], s_[:, l, 0:N], ALU.mult, r=[fk[2], fk[4]], w=[fk[2]])
                    if not samp:
                        P.op("dve", lambda l=l, oc=oc, N=N: nc.vector.tensor_tensor_scan(
                            out=h_[:, l, 0:N], data0=a_[:, l, 0:N], data1=i_[:, l, 0:N], initial=hst[:, j, oc:oc + 1],
                            op0=ALU.mult, op1=ALU.add), r=[fk[3], fk[2], "hst"], w=[fk[5]])
                        P.op("dve", lambda l=l, oc=oc, N=N: nc.vector.tensor_copy(
                            out=hst[:, j, oc:oc + 1], in_=h_[:, l, N - 1:N]), r=[fk[5]], w=["hst"], small=True)
                    else:
                        tt(h_[:, l, 0:N], a_[:, l, 0:N], h0_sb[:, j, oc, :], ALU.mult, r=[fk[3], "h0"], w=[fk[5]])
                        tt(h_[:, l, 0:N], h_[:, l, 0:N], i_[:, l, 0:N], ALU.add, r=[fk[5], fk[2]], w=[fk[5]])
                        dma("sp", hs_o[j, :, oc, :], h_[:, l, 0:N], r=[fk[5]], w=[])
                    stt(Y[:, oc, 0:N], h_[:, l, 0:N], 0.5, G[:, oc, 0:N], ALU.mult, ALU.mult, r=[fk[5], Gk], w=[Yk])
                    if dbg0 and b == 0:
                        dump("h%d" % l, h_[:, l, 0:N], fk[5])
            if dbg0:
                dump("Y", Y[:, :, 0:N], Yk)
            for oc in range(8):
                ps, pk = linear(wo, wok, oc * 128, Y[:, :, 0:N], Yk, N)
                add_resid(t["x"], t["xk"], oc, ps, pk, N)
            if dbg0:
                dump("xafter", t["x"], t["xk"])
        wrel(wgi)
        wrel(wxi)
        wrel(woi)
        if ci == 0:
            for tap in range(2):
                dma("sp", cs_o[j, tap], cv_sb[:, j, tap + 1, :, :], r=[("cvs", j)], w=[])
        if ci == NCHUNK - 1:
            dma("sp", hp_o[j], hst[:, j, :], r=["hst"], w=[])
            dma("sp", cp_o[j], xtail[:, j, :, :], r=["xtail"], w=[])

    def headnorm(ps, pk, N, gcol, out, okey, extra_out=None):
        act(sq_bf[:, 0, 0:N], ps[:, 0:N], AF.Square, r=[pk], w=["S8b"])
        ps2, pk2 = newps()
        mm(ps2[:, 0:N], blk_bf[:, :], sq_bf[:, 0, 0:N], True, True, r=["blk", "S8b"], w=[pk2])
        act(tmp1[:, 0:N], ps2[:, 0:N], AF.Ln, r=[pk2], w=["tmp1"], bias=EPS, scale=1.0 / 64)
        act(rstd[:, 0:N], tmp1[:, 0:N], AF.Exp, r=["tmp1"], w=["rstd"], scale=-0.5)
        stt(out, ps[:, 0:N], gcol, rstd[:, 0:N], ALU.mult, ALU.mult, r=[pk, "rstd", "prm", "gq8"], w=[okey])
        if extra_out is not None:
            eo, ek, c0 = extra_out
            stt(eo, ps[:, c0:N], gcol, rstd[:, c0:N], ALU.mult, ALU.mult, r=[pk, "rstd", "prm"], w=[ek])

    def swa(layer, tiles, ci):
        j = layer // 2
        for t in tiles:
            P.small_ctx = t["samp"]
            rmsnorm(t["x"], t["xk"], t["xn"], t["xnk"], t["N"], "gmix", layer)
        wq, wqk, wqi = wnext()
        wkv, wkvk, wkvi = wnext()
        wo, wok, woi = wnext()
        gqc = gq8[:, j:j + 1]
        gkc = pcol("gk", j)
        last = (ci == NCHUNK - 1)
        if ci > 0:
            P.op("pool", lambda: nc.gpsimd.tensor_copy(out=KT_sb[:, :, 0:128], in_=kthalo[j][:, :, :]),
                 r=[("kthalo", j)], w=["KT"])
            P.op("pool", lambda: nc.gpsimd.tensor_copy(out=V_sb[:, 0, :], in_=vhalo[j][:, :]),
                 r=[("vhalo", j)], w=["V"])
        for t in tiles:
            P.small_ctx = t["samp"]
            N = t["N"]
            if t["samp"]:
                swa_sample(j, t, wq, wqk, wkv, wkvk, wo, wok, gqc)
                continue
            st = t["st"]
            QT, OT = S8a, S8b
            for c in range(8):
                ps, pk = linear(wq, wqk, c * 128, t["xn"], t["xnk"], N)
                headnorm(ps, pk, N, gqc, QT[:, c, 0:N], "S8a")
            for g in range(4):
                ps, pk = linear(wkv, wkvk, g * 128, t["xn"], t["xnk"], N)
                extra = None
                if last and st == CH // ST - 1 and not CFG.get("nokp"):
                    extra = (kfin[:, g, :], "kfin", N - 128)
                headnorm(ps, pk, N, gkc, KT_sb[:, g, 128 + st * ST:128 + (st + 1) * ST], "KT", extra)
            for qb in range(ST // 128):
                n = st * (ST // 128) + qb
                ps, pk = newps()
                for k in range(8):
                    mm(ps[:, 0:256], t["xn"][:, k, qb * 128:(qb + 1) * 128], wkv[:, k, 512:768], k == 0, k == 7,
                       r=[t["xnk"], wkvk], w=[pk])
                act(V_sb[:, 1 + n, :], ps[:, 0:256], AF.Copy, r=[pk], w=["V"])
                if last and n == CH // 128 - 1 and not CFG.get("novp"):
                    act(vfin[:, :], ps[:, 0:256], AF.Copy, r=[pk], w=["vfin"])
            for qb in range(ST // 128):
                n = st * (ST // 128) + qb
                kbs = [1] if (ci == 0 and n == 0) else [0, 1]
                c0 = kbs[0] * 256
                for g in range(4):
                    pbk = ("p", g % 2)
                    rec = F4[2 + (g % 2)]
                    rk = ("F4", 2 + (g % 2))
                    pso, pko = newps()
                    psd, pkd = newps()
                    for hf in range(2):
                        ps, pk = newps()
                        eb = F4[hf][:, 0, :]
                        ebk = ("F4", hf)
                        pb = p_sb[g % 2][:, hf, :]
                        for kb in kbs:
                            kc0 = (n + kb) * 128
                            for c in range(2):
                                mm(ps[:, kb * 256 + c * 128:kb * 256 + (c + 1) * 128],
                                   KT_sb[hf * 64:(hf + 1) * 64, g, kc0:kc0 + 128],
                                   QT[hf * 64:(hf + 1) * 64, 2 * g + c, qb * 128:(qb + 1) * 128], True, True,
                                   r=["KT", "S8a"], w=[pk])
                        act(eb[:, c0:512], ps[:, c0:512], AF.Exp, r=[pk], w=[ebk])
                        ev = etab[:, kbs[0]:2, g, :].rearrange("p k (c h q) -> p k c h q", c=2, h=2)[:, :, :, hf, :]
                        tt(pb[:, c0:512].rearrange("p (k c q) -> p k c q", c=2, q=128),
                           eb[:, c0:512].rearrange("p (k c q) -> p k c q", c=2, q=128), ev, ALU.mult,
                           r=[ebk, "etab"], w=[pbk])
                        for ii, kb in enumerate(kbs):
                            mm(pso[0:64, hf * 256:(hf + 1) * 256], V_sb[:, n + kb, g * 64:(g + 1) * 64],
                               pb[:, kb * 256:(kb + 1) * 256], ii == 0, ii == len(kbs) - 1, r=["V", pbk], w=[pko])
                        for ii, kb in enumerate(kbs):
                            mm(psd[0:64, hf * 256:(hf + 1) * 256], ones_bf[:, 0:64],
                               pb[:, kb * 256:(kb + 1) * 256], ii == 0, ii == len(kbs) - 1, r=["ones", pbk], w=[pkd])
                    sv = sinkb[0:64, j, 4 * g:4 * g + 4].rearrange("p (c h) -> p h c", c=2).unsqueeze(3).to_broadcast(
                        [64, 2, 2, 128])
                    tt(rec[0:64, 0, :].rearrange("p (h c q) -> p h c q", h=2, c=2),
                       psd[0:64, :].rearrange("p (h c q) -> p h c q", h=2, c=2), sv, ALU.add, r=[pkd, "sinkb"], w=[rk])
                    P.op("dve", lambda rec=rec: nc.vector.reciprocal(out=rec[0:64, 0, :], in_=rec[0:64, 0, :]),
                         r=[rk], w=[rk])
                    for hf in range(2):
                        tt(OT[hf * 64:(hf + 1) * 64, 2 * g:2 * g + 2, qb * 128:(qb + 1) * 128],
                           pso[0:64, hf * 256:(hf + 1) * 256].rearrange("p (c q) -> p c q", c=2),
                           rec[0:64, 0, hf * 256:(hf + 1) * 256].rearrange("p (c q) -> p c q", c=2), ALU.mult,
                           r=[pko, rk], w=["S8b"])
            for oc in range(8):
                ps, pk = linear(wo, wok, oc * 128, OT[:, :, 0:N], "S8b", N)
                add_resid(t["x"], t["xk"], oc, ps, pk, N)
        wrel(wqi)
        wrel(wkvi)
        wrel(woi)
        if not last:
            P.op("pool", lambda: nc.gpsimd.tensor_copy(out=kthalo[j][:, :, :], in_=KT_sb[:, :, CH:CH + 128]),
                 r=["KT"], w=[("kthalo", j)])
            P.op("pool", lambda: nc.gpsimd.tensor_copy(out=vhalo[j][:, :], in_=V_sb[:, CH // 128, :]),
                 r=["V"], w=[("vhalo", j)])
        else:
            if not CFG.get("nokp"):
                dma("sp", kp_o[j], kfin[0:64, :, :], r=["kfin"], w=[])
            if not CFG.get("novp"):
                dma("sp", vp_o[j], vfin[:, :], r=["vfin"], w=[])

    def swa_sample(j, t, wq, wqk, wkv, wkvk, wo, wok, gqc):
        N = NS
        for c in range(8):
            ps, pk = linear(wq, wqk, c * 128, t["xn"], t["xnk"], N)
            headnorm(ps, pk, N, gqc, QTs[:, c, :], "QTs")
        psk, pkk = newps()
        kcols = lambda k: wkv[:, k, 0:512].rearrange("p (g two d) -> p g two d", g=4, two=2)[:, :, 0, :]
        for k in range(8):
            mm(psk[0:NS, 0:256].rearrange("p (g d) -> p g d", g=4), t["xn"][:, k, :], kcols(k), k == 0, k == 7,
               r=[t["xnk"], wkvk], w=[pkk])
        psv, pkv = newps()
        for k in range(8):
            mm(psv[0:NS, 0:256], t["xn"][:, k, :], wkv[:, k, 512:768], k == 0, k == 7, r=[t["xnk"], wkvk], w=[pkv])
        act(vtok[:, :], psv[0:NS, 0:256], AF.Copy, r=[pkv], w=["vtok"])
        act(ksq[:, :], psk[0:NS, 0:256], AF.Square, r=[pkk], w=["ksq"])
        P.op("dve", lambda: nc.vector.tensor_reduce(out=kss[:, :], in_=ksq[:, :].rearrange("p (g d) -> p g d", g=4),
                                                    op=ALU.add, axis=mybir.AxisListType.X), r=["ksq"], w=["kss"])
        act(kss[:, :], kss[:, :], AF.Ln, r=["kss"], w=["kss"], bias=EPS, scale=1.0 / 64)
        act(kss[:, :], kss[:, :], AF.Exp, r=["kss"], w=["kss"], scale=-0.5)
        tt(ktok[:, :].rearrange("p (g d) -> p g d", g=4), psk[0:NS, 0:256].rearrange("p (g d) -> p g d", g=4),
           kss[:, :].unsqueeze(2).to_broadcast([NS, 4, 64]), ALU.mult, r=[pkk, "kss"], w=["ktok"])
        gkrow = prm[0:NS, off["gkrow"] + j * 64: off["gkrow"] + (j + 1) * 64]
        tt(ktok[:, :].rearrange("p (g d) -> p g d", g=4), ktok[:, :].rearrange("p (g d) -> p g d", g=4),
           gkrow.unsqueeze(1).to_broadcast([NS, 4, 64]), ALU.mult, r=["ktok", "prm"], w=["ktok"])
        pso, pko = psA, "psA"
        psd, pkd = psB, "psB"
        for s in range(NS):
            Kn, Vn = Knew[s % 2], Vnew[s % 2]
            Kk, Vk = ("Knew", s % 2), ("Vnew", s % 2)
            dma("sp", Kn[0:127, :], ck[j, s, 1:128, :], r=[], w=[Kk])
            dma("sp", Kn[127:128, :], ktok[s:s + 1, :], r=["ktok"], w=[Kk])
            dma("sp", Vn[0:127, :], cv[j, s, 1:128, :], r=[], w=[Vk])
            dma("sp", Vn[127:128, :], vtok[s:s + 1, :], r=["vtok"], w=[Vk])
            dma("sp", ks_o[j, s], Kn[:, :], r=[Kk], w=[])
            dma("sp", vs_o[j, s], Vn[:, :], r=[Vk], w=[])
            P.op("dve", lambda Kn=Kn: nc.vector.tensor_copy(
                out=Kdup[:, :, :, :], in_=Kn[:, :].rearrange("p (g d) -> p g d", g=4).unsqueeze(2).to_broadcast(
                    [128, 4, 2, 64])), r=[Kk], w=["Kdup"])
            P.op("dve", lambda Vn=Vn: nc.vector.tensor_copy(out=Vb[:, :], in_=Vn[:, :]), r=[Vk], w=["Vb"])
            for g in range(4):
                P.op("pe", lambda g=g: nc.tensor.transpose(
                    out=psT[:, g, :], in_=Kdup[:, g, :, :].rearrange("p a d -> p (a d)"), identity=ident_bf[:, :]),
                    r=["Kdup", "ident"], w=["psT"])
            act(KTs[:, :, :], psT[:, :, :], AF.Copy, r=["psT"], w=["KTs"])
            pssX, pksX = newps()
            pssY, pksY = newps()
            for hf, (pss, pks) in enumerate(((pssX, pksX), (pssY, pksY))):
                for g in range(4):
                    mm(pss[:, 2 * g:2 * g + 2], KTs[hf * 64:(hf + 1) * 64, g, :],
                       QTs[hf * 64:(hf + 1) * 64, 2 * g:2 * g + 2, s], True, True, r=["KTs", "QTs"], w=[pks])
                act(es_sb[:, hf * 8:(hf + 1) * 8], pss[:, 0:8], AF.Exp, r=[pks], w=["es"])
            esv = prm[:, off["es"]:off["es"] + 16].rearrange("p (k h) -> p h k", h=2)
            tt(ps_s[:, :].rearrange("p (h k) -> p h k", h=2), es_sb[:, :].rearrange("p (h k) -> p h k", h=2), esv,
               ALU.mult, r=["es", "prm"], w=["pss"])
            for g in range(4):
                mm(pso[0:64, s * 16:(s + 1) * 16].rearrange("p (h g c) -> p h g c", h=2, g=4)[:, :, g, :],
                   Vb[:, g * 64:(g + 1) * 64],
                   ps_s[:, :].rearrange("p (h g c) -> p h g c", h=2, g=4)[:, :, g, :],
                   True, True, r=["Vb", "pss"], w=[pko])
            mm(psd[0:64, s * 16:(s + 1) * 16], ones_bf[:, 0:64], ps_s[:, :], True, True, r=["ones", "pss"], w=[pkd])
        rec = F4[2]
        rk = ("F4", 2)
        sv = sinkb[0:64, j, :].rearrange("p (k h) -> p h k", h=2).unsqueeze(1).to_broadcast([64, NS, 2, 8])
        tt(rec[0:64, 0, 0:256].rearrange("p (s h k) -> p s h k", s=NS, h=2),
           psd[0:64, 0:256].rearrange("p (s h k) -> p s h k", s=NS, h=2), sv, ALU.add, r=[pkd, "sinkb"], w=[rk])
        P.op("dve", lambda: nc.vector.reciprocal(out=rec[0:64, 0, 0:256], in_=rec[0:64, 0, 0:256]), r=[rk], w=[rk])
        for hf in range(2):
            o_v = pso[0:64, 0:256].rearrange("p (s h k) -> p h k s", s=NS, h=2)[:, hf, :, :]
            r_v = rec[0:64, 0, 0:256].rearrange("p (s h k) -> p h k s", s=NS, h=2)[:, hf, :, :]
            tt(OTs[hf * 64:(hf + 1) * 64, :, :], o_v, r_v, ALU.mult, r=[pko, rk], w=["OTs"])
        for oc in range(8):
            ps, pk = linear(wo, wok, oc * 128, OTs[:, :, :], "OTs", N)
            add_resid(t["x"], t["xk"], oc, ps, pk, N)

    for ci in range(CFG["nchunk"]):
        dma("sp", x_sb[:, :, :], xT[:, ci * CH:(ci + 1) * CH].rearrange("(k p) n -> p k n", p=128), r=[], w=["x"])
        tiles = tiles_for_chunk(ci)
        P.small_ctx = False
        for layer in CFG["layers"]:
            if CFG["mixer"]:
                if layer % 2 == 0:
                    rglru(layer, tiles, ci)
                else:
                    swa(layer, tiles, ci)
            if CFG["mlp"]:
                mlp(layer, tiles)
        dma("sp", yT[:, ci * CH:(ci + 1) * CH].rearrange("(k p) n -> p k n", p=128), x_sb[:, :, :], r=["x"], w=[])
        if ci == 0:
            dma("sp", ysT.rearrange("(k p) n -> p k n", p=128), xs_sb[:, :, :], r=["xs"], w=[])
    P.emit()
    nc._prog_stats = P.stats
    nc._dumps = dumps
    return nc


_NC_CACHE = {}


def _slopes():
    h = np.arange(1, 17, dtype=np.float32)
    return np.exp2(-8.0 * h / 16.0).astype(np.float32)


def _etab():
    sl = _slopes().astype(np.float64)
    k = np.arange(128)[:, None]
    q = np.arange(128)[None, :]
    tab = np.zeros((128, 2, 4, 4, 128), np.float32)
    for g in range(4):
        for jj in range(4):
            h = 4 * g + jj
            dprev = q + 128 - k
            dcur = q - k
            tab[:, 0, g, jj, :] = np.where(dprev < 128, np.exp(-sl[h] * dprev), 0.0)
            tab[:, 1, g, jj, :] = np.where(dcur >= 0, np.exp(-sl[h] * np.maximum(dcur, 0)), 0.0)
    return tab.reshape(128, 2 * 4 * 512)


def kernel(x_prompt, x_sample, state_rglru_h, state_rglru_conv, cache_swa_k, cache_swa_v,
           norm_mix_g, norm_mlp_g,
           lru_w_in, lru_conv_w, lru_conv_b, lru_w_a, lru_b_a, lru_w_x, lru_b_x, lru_lambda, lru_w_out,
           attn_w_qkv, attn_q_norm, attn_k_norm, attn_sinks, attn_w_out,
           mlp_w_up, mlp_w_down):
    f = lambda a: np.ascontiguousarray(np.asarray(a, dtype=np.float32))
    x_prompt, x_sample = f(x_prompt), f(x_sample)
    if "nc" not in _NC_CACHE:
        _NC_CACHE["nc"] = build_program()
    nc = _NC_CACHE["nc"]

    w_in = f(lru_w_in)
    wa, wx = f(lru_w_a), f(lru_w_x)
    wg = np.stack([wa, wx], axis=1)
    wg = wg.reshape(2, 2, 4, 2, 128, 256).transpose(0, 4, 1, 2, 3, 5).reshape(2, 128, 16, 256)
    w_gt = f(wg)
    w_lo = f(lru_w_out)
    wqkv = f(attn_w_qkv)
    w_q = f(wqkv[:, :, 0:1024])
    kpart = wqkv[:, :, 1024:1280].reshape(2, D, 4, 1, 64)
    kdup = np.broadcast_to(kpart, (2, D, 4, 2, 64)).reshape(2, D, 512)
    w_kv = f(np.concatenate([kdup, wqkv[:, :, 1280:1536]], axis=2))
    w_ao = f(attn_w_out)
    w_up, w_dn = f(mlp_w_up), f(mlp_w_down)

    def pk8(a):
        a = f(a)
        lead = a.shape[:-1]
        return np.moveaxis(a.reshape(*lead, 8, 128), -1, 0)

    cols = []
    cols.append(pk8(norm_mix_g).reshape(128, 32))
    cols.append(pk8(norm_mlp_g).reshape(128, 32))
    cw = pk8(lru_conv_w)
    cols.append(cw.transpose(0, 1, 3, 2).reshape(128, 64))
    cols.append(pk8(lru_conv_b).reshape(128, 16))
    cols.append(pk8(lru_b_a).reshape(128, 16))
    cols.append(pk8(lru_b_x).reshape(128, 16))
    cols.append(pk8(lru_lambda).reshape(128, 16))
    qn, kn = f(attn_q_norm), f(attn_k_norm)
    cols.append(np.concatenate([qn, qn], axis=1).T)
    cols.append(np.concatenate([kn, kn], axis=1).T)
    cols.append(np.broadcast_to(kn.reshape(1, 128), (128, 128)))
    sl = _slopes().astype(np.float64)
    kk = np.arange(128)[:, None]
    cols.append(np.exp(-sl[None, :] * (127 - kk)).astype(np.float32))
    prm = f(np.concatenate([np.asarray(c, np.float32) for c in cols], axis=1))
    etab = _etab()
    sinks = f(attn_sinks).reshape(1, 32)

    h_all, cv_all = f(state_rglru_h), f(state_rglru_conv)
    ck_all, cvv_all = f(cache_swa_k), f(cache_swa_v)
    xTs = [f(x_prompt[b].T) for b in range(2)]
    in_maps = []
    for c in range(8):
        s0 = c * NS
        m = dict(
            xT=xTs[c % 2],
            xsT=f(x_sample[s0:s0 + NS, 0, :].T),
            h0T=f(h_all[:, s0:s0 + NS, :].reshape(2, NS, 8, 128).transpose(0, 3, 2, 1)),
            cvT=f(cv_all[:, s0:s0 + NS].reshape(2, NS, 3, 8, 128).transpose(0, 2, 4, 3, 1)),
            ck=f(ck_all[:, s0:s0 + NS].reshape(2, NS, 128, 256)),
            cv=f(cvv_all[:, s0:s0 + NS].reshape(2, NS, 128, 256)),
            w_in=w_in, w_gt=w_gt, w_lo=w_lo, w_q=w_q, w_kv=w_kv, w_ao=w_ao, w_up=w_up, w_dn=w_dn,
            prm=prm, etab=etab, sinks=sinks, ident=np.eye(128, dtype=np.float32),
        )
        in_maps.append(m)
    res = run_bass_kernel_spmd(nc, in_maps, core_ids=list(range(8)))
    R = res.results
    if getattr(nc, "_dumps", None):
        _NC_CACHE["dumps"] = {n: R[0][n] for n in nc._dumps}

    y_prompt = np.stack([R[b]["yT"].T for b in range(2)]).astype(np.float32)
    y_sample = np.concatenate([R[c]["ysT"].T for c in range(8)], axis=0).reshape(128, 1, D).astype(np.float32)
    h_p = np.stack([R[b]["hp"].transpose(0, 2, 1).reshape(2, D) for b in range(2)], axis=1)
    c_p = np.stack([R[b]["cp"].transpose(0, 3, 2, 1).reshape(2, 3, D) for b in range(2)], axis=1)
    k_p = np.stack([R[b]["kp"].transpose(0, 3, 2, 1) for b in range(2)], axis=1)
    v_p = np.stack([R[b]["vp"].reshape(2, 128, 4, 64) for b in range(2)], axis=1)
    h_s = np.concatenate([R[c]["hs"].transpose(0, 3, 2, 1).reshape(2, NS, D) for c in range(8)], axis=1)
    c_s = np.concatenate([R[c]["cs"].transpose(0, 4, 1, 3, 2).reshape(2, NS, 3, D) for c in range(8)], axis=1)
    k_s = np.concatenate([R[c]["ks"].reshape(2, NS, 128, 4, 64) for c in range(8)], axis=1)
    v_s = np.concatenate([R[c]["vs"].reshape(2, NS, 128, 4, 64) for c in range(8)], axis=1)
    c32 = lambda a: np.ascontiguousarray(a, dtype=np.float32)
    return (c32(y_prompt), c32(y_sample), c32(h_p), c32(c_p), c32(k_p), c32(v_p),
            c32(h_s), c32(c_s), c32(k_s), c32(v_s))
```
